# Optimizing a Trainium2 kernel written in Bass

```python
import jax
import jax.numpy as jnp
from jax import lax
import numpy as np

D_MODEL = 2048
BATCH = 16
SEQ = 2048
DEPTH = 4
DEC_BATCH = 4
DEC_SEQ = 8192
PAST_LEN = 128

GRID_W = 64
HEAD_DIM = 128
EPS = 1e-6
NEG_INF = -1e30

NA_HEADS = 8
NA_WIN_R = 8
NA_WIN_C = 16
NA_COL_BLOCK = 16
NA_COL_SPAN = 32

MLA_HEADS = 8
MLA_Q_RANK = 512
MLA_KV_RANK = 256
MLA_NOPE = 128
MLA_ROPE = 64
MLA_V = 128
MLA_QK = MLA_NOPE + MLA_ROPE
MLA_Q_BLOCK = 128
ROPE_THETA = 10000.0

DIL_GROUPS = ((128, 1), (512, 4), (2048, 16))
DIL_HEADS_PER_GROUP = 4
DIL_HEADS = 12
DIL_Q_BLOCK = 64

D_FF = 5632

NA_WIDTH = NA_HEADS * HEAD_DIM
DIL_WIDTH = DIL_HEADS * HEAD_DIM
DIL_OUT = DIL_HEADS_PER_GROUP * HEAD_DIM
MLA_OUT = MLA_HEADS * MLA_V
SPLITS = (NA_WIDTH, NA_WIDTH, NA_WIDTH, MLA_Q_RANK, MLA_KV_RANK, MLA_ROPE,
          DIL_WIDTH, DIL_WIDTH, DIL_WIDTH, D_MODEL, D_MODEL, D_MODEL)
P_IN = 3 * NA_WIDTH + MLA_Q_RANK + MLA_KV_RANK + MLA_ROPE + 3 * DIL_WIDTH + 3 * D_MODEL

kernel_name = "hybrid_bidir_encoder_na_mla_dilated"


def rms_norm(x, g):
    xf = x.astype(jnp.float32)
    y = xf * lax.rsqrt(jnp.mean(xf * xf, axis=-1, keepdims=True) + EPS)
    return (y * g.astype(jnp.float32)).astype(x.dtype)


def swiglu(x, w_gate, w_up, w_down):
    return (jax.nn.silu(x @ w_gate) * (x @ w_up)) @ w_down


def to_heads(t, n_heads, dh):
    b, s, _ = t.shape
    return t.reshape(b, s, n_heads, dh).transpose(0, 2, 1, 3)


def from_heads(t):
    b, h, s, d = t.shape
    return t.transpose(0, 2, 1, 3).reshape(b, s, h * d)


def neighbourhood_attention(q, k, v, rpb):
    b, h, s, hd = q.shape
    rows = s // GRID_W
    kr = min(NA_WIN_R, rows)
    n_cb = GRID_W // NA_COL_BLOCK
    qg = q.reshape(b, h, rows, n_cb, NA_COL_BLOCK, hd)
    kg = k.reshape(b, h, rows, GRID_W, hd)
    vg = v.reshape(b, h, rows, GRID_W, hd)
    qcol = np.arange(GRID_W).reshape(n_cb, NA_COL_BLOCK)
    col_start = np.clip(qcol - NA_WIN_C // 2, 0, GRID_W - NA_WIN_C)
    span_start = np.clip(np.arange(n_cb) * NA_COL_BLOCK - NA_WIN_C // 2, 0, GRID_W - NA_COL_SPAN)
    kcol = span_start[:, None] + np.arange(NA_COL_SPAN)[None, :]
    dc = kcol[:, None, :] - qcol[:, :, None]
    col_ok = (kcol[:, None, :] >= col_start[:, :, None]) & (kcol[:, None, :] < col_start[:, :, None] + NA_WIN_C)
    dc_idx = np.clip(dc, -(NA_WIN_C - 1), NA_WIN_C - 1) + NA_WIN_C - 1
    rpb_c = rpb[:, :, dc_idx]
    col_ok_j = jnp.asarray(col_ok)[:, :, None, :]
    kcol_j = jnp.asarray(kcol)
    scale = HEAD_DIM ** -0.5

    def row_block(r):
        r0 = jnp.clip(r - kr // 2, 0, rows - kr)
        k_rows = lax.dynamic_slice_in_dim(kg, r0, kr, axis=2)
        v_rows = lax.dynamic_slice_in_dim(vg, r0, kr, axis=2)
        k_blk = jnp.take(k_rows, kcol_j, axis=3)
        v_blk = jnp.take(v_rows, kcol_j, axis=3)
        q_row = lax.dynamic_index_in_dim(qg, r, axis=2, keepdims=False)
        sc = jnp.einsum('bhcqd,bhicmd->bhcqim', q_row, k_blk).astype(jnp.float32) * scale
        dr_idx = r0 + jnp.arange(kr) - r + NA_WIN_R - 1
        bias = jnp.take(rpb_c, dr_idx, axis=1).transpose(0, 2, 3, 1, 4)
        sc = jnp.where(col_ok_j, sc + bias.astype(jnp.float32), NEG_INF)
        p = jax.nn.softmax(sc.reshape(b, h, n_cb, NA_COL_BLOCK, kr * NA_COL_SPAN), axis=-1)
        p = p.reshape(sc.shape).astype(v.dtype)
        o = jnp.einsum('bhcqim,bhicmd->bhcqd', p, v_blk)
        return o.reshape(b, h, GRID_W, hd)

    out = lax.map(row_block, jnp.arange(rows))
    return out.transpose(1, 2, 0, 3, 4).reshape(b, h, s, hd)


def rope(x, pos):
    half = x.shape[-1] // 2
    inv = ROPE_THETA ** (-jnp.arange(half, dtype=jnp.float32) / half)
    ang = pos.astype(jnp.float32)[:, None] * inv[None, :]
    cos, sin = jnp.cos(ang), jnp.sin(ang)
    xf = x.astype(jnp.float32)
    x1, x2 = xf[..., :half], xf[..., half:]
    return jnp.concatenate([x1 * cos - x2 * sin, x1 * sin + x2 * cos], axis=-1).astype(x.dtype)


def mla_attention(c_q, c_kv, k_rope, cq_norm, ckv_norm, w_uq, w_ukv, q_norm, k_norm):
    b, s, _ = c_q.shape
    pos = jnp.arange(s)
    q = to_heads(rms_norm(c_q, cq_norm) @ w_uq, MLA_HEADS, MLA_QK)
    kv = to_heads(rms_norm(c_kv, ckv_norm) @ w_ukv, MLA_HEADS, MLA_NOPE + MLA_V)
    k_nope, v = kv[..., :MLA_NOPE], kv[..., MLA_NOPE:]
    q = jnp.concatenate([q[..., :MLA_NOPE], rope(q[..., MLA_NOPE:], pos)], axis=-1)
    k_r = jnp.broadcast_to(rope(k_rope, pos)[:, None], (b, MLA_HEADS, s, MLA_ROPE))
    k = jnp.concatenate([k_nope, k_r], axis=-1)
    q = rms_norm(q, q_norm)
    k = rms_norm(k, k_norm)
    n_blk = s // MLA_Q_BLOCK
    qb = q.reshape(b, MLA_HEADS, n_blk, MLA_Q_BLOCK, MLA_QK).transpose(2, 0, 1, 3, 4)
    scale = MLA_QK ** -0.5

    def block(q_blk):
        sc = jnp.einsum('bhqd,bhkd->bhqk', q_blk, k).astype(jnp.float32) * scale
        p = jax.nn.softmax(sc, axis=-1).astype(v.dtype)
        return jnp.einsum('bhqk,bhkd->bhqd', p, v)

    o = lax.map(block, qb)
    return o.transpose(1, 0, 3, 2, 4).reshape(b, s, MLA_OUT)


def alibi_slopes(n):
    return 2.0 ** (-8.0 * jnp.arange(1, n + 1, dtype=jnp.float32) / n)


def banded_attention(q, k, v, slope, radius):
    b, h, r, l, hd = q.shape
    n_blk = -(-l // DIL_Q_BLOCK)
    lp = n_blk * DIL_Q_BLOCK
    kb_len = DIL_Q_BLOCK + 2 * radius
    lead = [(0, 0)] * 3
    qp = jnp.pad(q, lead + [(0, lp - l), (0, 0)])
    kp = jnp.pad(k, lead + [(radius, lp - l + radius), (0, 0)])
    vp = jnp.pad(v, lead + [(radius, lp - l + radius), (0, 0)])
    key_idx = np.arange(n_blk)[:, None] * DIL_Q_BLOCK + np.arange(kb_len)[None, :]
    kb = jnp.take(kp, key_idx, axis=3)
    vb = jnp.take(vp, key_idx, axis=3)
    qb = qp.reshape(b, h, r, n_blk, DIL_Q_BLOCK, hd)
    qpos = np.arange(lp).reshape(n_blk, DIL_Q_BLOCK)
    kpos = key_idx - radius
    dist = np.abs(qpos[:, :, None] - kpos[:, None, :])
    valid = (kpos[:, None, :] >= 0) & (kpos[:, None, :] < l) & (dist <= radius)
    sc = jnp.einsum('bhrnqd,bhrnkd->bhrnqk', qb, kb).astype(jnp.float32) * HEAD_DIM ** -0.5
    sc = sc - slope.astype(jnp.float32)[None, :, None, None, None, None] * jnp.asarray(dist, jnp.float32)
    sc = jnp.where(jnp.asarray(valid), sc, NEG_INF)
    m = jnp.max(sc, axis=-1, keepdims=True)
    p = jnp.exp(sc - m)
    den = jnp.sum(p, axis=-1, keepdims=True)
    o = jnp.einsum('bhrnqk,bhrnkd->bhrnqd', p.astype(v.dtype), vb).astype(jnp.float32) / den
    lse = (m + jnp.log(den))[..., 0]
    o = o.reshape(b, h, r, lp, hd)[:, :, :, :l]
    lse = lse.reshape(b, h, r, lp)[:, :, :, :l]
    return o, lse


def dilated_attention(q, k, v):
    b, h, s, hd = q.shape
    slopes = alibi_slopes(DIL_HEADS)
    outs, lses = [], []
    for g, (window, dil) in enumerate(DIL_GROUPS):
        lo, hi = g * DIL_HEADS_PER_GROUP, (g + 1) * DIL_HEADS_PER_GROUP
        n = s // dil
        qr = q[:, lo:hi].reshape(b, DIL_HEADS_PER_GROUP, n, dil, hd).transpose(0, 1, 3, 2, 4)
        kr = k[:, lo:hi].reshape(b, DIL_HEADS_PER_GROUP, n, dil, hd).transpose(0, 1, 3, 2, 4)
        vr = v[:, lo:hi].reshape(b, DIL_HEADS_PER_GROUP, n, dil, hd).transpose(0, 1, 3, 2, 4)
        o, lse = banded_attention(qr, kr, vr, slopes[lo:hi] * dil, (window // 2) // dil)
        outs.append(o.transpose(0, 1, 3, 2, 4).reshape(b, DIL_HEADS_PER_GROUP, s, hd))
        lses.append(lse.transpose(0, 1, 3, 2).reshape(b, DIL_HEADS_PER_GROUP, s))
    wts = jax.nn.softmax(jnp.stack(lses), axis=0)
    o = jnp.sum(wts[..., None] * jnp.stack(outs), axis=0)
    return from_heads(o).astype(q.dtype)


def token_mixing(h, w_in, na_q_norm, na_k_norm, na_rpb, mla_cq_norm, mla_ckv_norm, mla_w_uq,
                 mla_w_ukv, mla_q_norm, mla_k_norm, dil_q_norm, dil_k_norm,
                 w_na_out, w_mla_out, w_dil_out, w_o):
    z = h @ w_in
    (na_q, na_k, na_v, c_q, c_kv, k_rope, dq, dk, dv, g_na, g_mla, g_dil) = jnp.split(
        z, np.cumsum(SPLITS)[:-1].tolist(), axis=-1)
    qa = rms_norm(to_heads(na_q, NA_HEADS, HEAD_DIM), na_q_norm)
    ka = rms_norm(to_heads(na_k, NA_HEADS, HEAD_DIM), na_k_norm)
    va = to_heads(na_v, NA_HEADS, HEAD_DIM)
    y_na = from_heads(neighbourhood_attention(qa, ka, va, na_rpb)) @ w_na_out
    y_mla = mla_attention(c_q, c_kv, k_rope, mla_cq_norm, mla_ckv_norm, mla_w_uq, mla_w_ukv,
                          mla_q_norm, mla_k_norm) @ w_mla_out
    qd = rms_norm(to_heads(dq, DIL_HEADS, HEAD_DIM), dil_q_norm)
    kd = rms_norm(to_heads(dk, DIL_HEADS, HEAD_DIM), dil_k_norm)
    vd = to_heads(dv, DIL_HEADS, HEAD_DIM)
    y_dil = dilated_attention(qd, kd, vd) @ w_dil_out
    merged = jax.nn.sigmoid(g_na) * y_na + jax.nn.sigmoid(g_mla) * y_mla + jax.nn.sigmoid(g_dil) * y_dil
    return merged @ w_o


def encoder_layer(x, ffn1_norm, ffn1_w_gate, ffn1_w_up, ffn1_w_down, mix_norm, w_in,
                  na_q_norm, na_k_norm, na_rpb, mla_cq_norm, mla_ckv_norm, mla_w_uq, mla_w_ukv,
                  mla_q_norm, mla_k_norm, dil_q_norm, dil_k_norm, w_na_out, w_mla_out, w_dil_out,
                  w_o, ffn2_norm, ffn2_w_gate, ffn2_w_up, ffn2_w_down):
    x = x + 0.5 * swiglu(rms_norm(x, ffn1_norm), ffn1_w_gate, ffn1_w_up, ffn1_w_down)
    x = x + token_mixing(rms_norm(x, mix_norm), w_in, na_q_norm, na_k_norm, na_rpb,
                         mla_cq_norm, mla_ckv_norm, mla_w_uq, mla_w_ukv, mla_q_norm, mla_k_norm,
                         dil_q_norm, dil_k_norm, w_na_out, w_mla_out, w_dil_out, w_o)
    x = x + 0.5 * swiglu(rms_norm(x, ffn2_norm), ffn2_w_gate, ffn2_w_up, ffn2_w_down)
    return x


def trunk(x, params):
    for layer in range(DEPTH):
        x = encoder_layer(x, *[p[layer] for p in params])
    return x


def setup_inputs(seed: int = 0) -> dict:
    key = jax.random.key(seed)
    ks = jax.random.split(key, 32)
    f32 = jnp.float32
    L = DEPTH

    def w(k, shape, fan_in):
        return jax.random.normal(k, shape, f32) * fan_in ** -0.5

    def gain(k, shape):
        return 1.0 + 0.02 * jax.random.normal(k, shape, f32)

    return {
        "x_prompt": jax.random.normal(ks[0], (BATCH, SEQ, D_MODEL), f32),
        "x_sample": jax.random.normal(ks[1], (DEC_BATCH, DEC_SEQ, D_MODEL), f32),
        "ffn1_norm": gain(ks[2], (L, D_MODEL)),
        "ffn1_w_gate": w(ks[3], (L, D_MODEL, D_FF), D_MODEL),
        "ffn1_w_up": w(ks[4], (L, D_MODEL, D_FF), D_MODEL),
        "ffn1_w_down": w(ks[5], (L, D_FF, D_MODEL), D_FF),
        "mix_norm": gain(ks[6], (L, D_MODEL)),
        "w_in": w(ks[7], (L, D_MODEL, P_IN), D_MODEL),
        "na_q_norm": gain(ks[8], (L, HEAD_DIM)),
        "na_k_norm": gain(ks[9], (L, HEAD_DIM)),
        "na_rpb": 0.1 * jax.random.normal(ks[10], (L, NA_HEADS, 2 * NA_WIN_R - 1, 2 * NA_WIN_C - 1), f32),
        "mla_cq_norm": gain(ks[11], (L, MLA_Q_RANK)),
        "mla_ckv_norm": gain(ks[12], (L, MLA_KV_RANK)),
        "mla_w_uq": w(ks[13], (L, MLA_Q_RANK, MLA_HEADS * MLA_QK), MLA_Q_RANK),
        "mla_w_ukv": w(ks[14], (L, MLA_KV_RANK, MLA_HEADS * (MLA_NOPE + MLA_V)), MLA_KV_RANK),
        "mla_q_norm": gain(ks[15], (L, MLA_QK)),
        "mla_k_norm": gain(ks[16], (L, MLA_QK)),
        "dil_q_norm": gain(ks[17], (L, HEAD_DIM)),
        "dil_k_norm": gain(ks[18], (L, HEAD_DIM)),
        "w_na_out": w(ks[19], (L, NA_WIDTH, D_MODEL), NA_WIDTH),
        "w_mla_out": w(ks[20], (L, MLA_OUT, D_MODEL), MLA_OUT),
        "w_dil_out": w(ks[21], (L, DIL_OUT, D_MODEL), DIL_OUT),
        "w_o": w(ks[22], (L, D_MODEL, D_MODEL), D_MODEL),
        "ffn2_norm": gain(ks[23], (L, D_MODEL)),
        "ffn2_w_gate": w(ks[24], (L, D_MODEL, D_FF), D_MODEL),
        "ffn2_w_up": w(ks[25], (L, D_MODEL, D_FF), D_MODEL),
        "ffn2_w_down": w(ks[26], (L, D_FF, D_MODEL), D_FF),
    }


def reference(x_prompt, x_sample, ffn1_norm, ffn1_w_gate, ffn1_w_up, ffn1_w_down, mix_norm, w_in,
              na_q_norm, na_k_norm, na_rpb, mla_cq_norm, mla_ckv_norm, mla_w_uq, mla_w_ukv,
              mla_q_norm, mla_k_norm, dil_q_norm, dil_k_norm, w_na_out, w_mla_out, w_dil_out,
              w_o, ffn2_norm, ffn2_w_gate, ffn2_w_up, ffn2_w_down):
    params = (ffn1_norm, ffn1_w_gate, ffn1_w_up, ffn1_w_down, mix_norm, w_in,
              na_q_norm, na_k_norm, na_rpb, mla_cq_norm, mla_ckv_norm, mla_w_uq, mla_w_ukv,
              mla_q_norm, mla_k_norm, dil_q_norm, dil_k_norm, w_na_out, w_mla_out, w_dil_out,
              w_o, ffn2_norm, ffn2_w_gate, ffn2_w_up, ffn2_w_down)
    y_prompt = trunk(x_prompt, params)
    y_sample = trunk(x_sample, params)
    return (y_prompt, y_sample)
```

```python
import math
import numpy as np
import ml_dtypes
import concourse.bass as bass
import concourse.mybir as mybir
from concourse.bass_utils import run_bass_kernel_spmd

F32 = mybir.dt.float32
BF16 = mybir.dt.bfloat16
AF = mybir.ActivationFunctionType
ALU = mybir.AluOpType
NPBF = ml_dtypes.bfloat16

D = 2048
DFF = 5632
KC = 16
FC = 44
TT = 512
EPS = 1e-6
NEG = -30000.0
BIGD = 1.0e7
N_CORES = 8
NTOK_FULL = 8192
DEPTH_FULL = 4
NU = 200

C_NAQ, C_NAK, C_NAV, C_CQ, C_CKV, C_KR = 0, 1024, 2048, 3072, 3584, 3840
C_DQ, C_DK, C_DV, C_G = 3904, 5440, 6976, 8512

DIL_D = (1, 4, 16)
DIL_DELTAS = (list(range(-1, 5)), list(range(-2, 6)), list(range(-8, 12)))
N_DD = sum(len(x) for x in DIL_DELTAS)
SLOPES = [2.0 ** (-8.0 * (i + 1) / 12.0) for i in range(12)]

G_F1, G_MIX, G_F2 = 0, 16, 32
G_NAQ, G_NAK, G_CQ, G_CKV, G_MQ, G_MK, G_DQ, G_DK = 48, 49, 50, 54, 56, 58, 60, 61
NG = 62


def weight_tiles():
    tl = []
    r = lambda a, n: list(range(a, a + n))
    for f in ("ffn1", "ffn2"):
        for t in range(11):
            tl.append((f"{f}g{t}", f + "_w_gate", 16, r(512 * t, 512)))
            tl.append((f"{f}u{t}", f + "_w_up", 16, r(512 * t, 512)))
        for d in range(16):
            tl.append((f"{f}d{d}", f + "_w_down", 44, r(128 * d, 128)))
    for t in range(4):
        tl.append((f"qkna{t}", "w_in", 16, r(512 * t, 512)))
    for t in range(2):
        tl.append((f"vna{t}", "w_in", 16, r(C_NAV + 512 * t, 512)))
    tl.append(("cq", "w_in", 16, r(C_CQ, 512)))
    tl.append(("uqn", "mla_w_uq", 4, [h * 192 + i for h in range(8) for i in range(128)]))
    tl.append(("uqr", "mla_w_uq", 4, [h * 192 + 128 + i for h in range(8) for i in range(64)]))
    tl.append(("uqt", "mla_w_uq", 4, [h * 192 + 128 + (i + 32) % 64 for h in range(8) for i in range(64)]))
    tl.append(("ckv", "w_in", 16, r(C_CKV, 256)))
    tl.append(("krr", "w_in", 16, r(C_KR, 64) + [C_KR + (i + 32) % 64 for i in range(64)]))
    tl.append(("ukvk", "mla_w_ukv", 2, [h * 256 + i for h in range(8) for i in range(128)]))
    tl.append(("ukvv", "mla_w_ukv", 2, [h * 256 + 128 + i for h in range(8) for i in range(128)]))
    for t in range(6):
        tl.append((f"qkdil{t}", "w_in", 16, r(C_DQ + 512 * t, 512)))
    for t in range(3):
        tl.append((f"vdil{t}", "w_in", 16, r(C_DV + 512 * t, 512)))
    for t in range(12):
        tl.append((f"gate{t}", "w_in", 16, r(C_G + 512 * t, 512)))
    for t in range(8):
        tl.append((f"mo{t}", "MO", 20, r(256 * t, 256)))
    for t in range(4):
        tl.append((f"wo{t}", "w_o", 16, r(512 * t, 512)))
    return tl


_WT = weight_tiles()
_WOFF = {}
_o = 0
for (_n, _s, _kc, _cols) in _WT:
    _WOFF[_n] = (_o, _kc, len(_cols))
    _o += _kc * len(_cols)
WTOT = _o


def pack_weights_layer(inp, l):
    out = np.empty((128, WTOT), np.float32)
    mo = None
    for (name, src, kc, cols) in _WT:
        if src == "MO":
            if mo is None:
                mo = np.concatenate([inp["w_na_out"][l], inp["w_mla_out"][l], inp["w_dil_out"][l]], axis=0)
            w = mo
        else:
            w = inp[src][l]
        off, _, nc_ = _WOFF[name]
        cols = np.asarray(cols)
        if np.all(np.diff(cols) == 1):
            sub = w[: kc * 128, cols[0]: cols[0] + len(cols)]
        else:
            sub = w[: kc * 128][:, cols]
        out[:, off: off + kc * nc_] = sub.reshape(kc, 128, nc_).transpose(1, 0, 2).reshape(128, kc * nc_)
    return out


class Eng:
    def __init__(self, name, key, is_pe=False):
        self.name, self.key, self.is_pe = name, key, is_pe
        self.count = 0
        self.waited = {}
        self.ops = []
        self.dma_i = 0
        self.pool = []


class Tracker:
    def __init__(self):
        self.res = {}
        self.engs_by_key = {}

    def op(self, E, fn, reads=(), writes=(), signal=False, dma=False, extra=()):
        deps = {}
        res = self.res
        for r in reads:
            st = res.get(r)
            if st is not None and st[0] is not None:
                s, v = st[0]
                if deps.get(s, 0) < v:
                    deps[s] = v
        for w in writes:
            st = res.get(w)
            if st is not None:
                if st[0] is not None:
                    s, v = st[0]
                    if deps.get(s, 0) < v:
                        deps[s] = v
                for s, v in st[1].items():
                    if deps.get(s, 0) < v:
                        deps[s] = v
        for (s, v) in extra:
            if deps.get(s, 0) < v:
                deps[s] = v
        if dma:
            i = E.dma_i
            E.dma_i += 1
            np_ = len(E.pool)
            slot, rnd = i % np_, i // np_
            sk = E.pool[slot]
            if rnd > 0 and deps.get(sk, 0) < 16 * rnd:
                deps[sk] = 16 * rnd
            tok = (sk, 16 * (rnd + 1))
        else:
            tok = (E.key, len(E.ops) + 1)
        waits = []
        for s, v in deps.items():
            if s == E.key and E.is_pe:
                continue
            if E.waited.get(s, 0) < v:
                E.waited[s] = v
                waits.append((s, v))
                F = self.engs_by_key.get(s)
                if F is not None:
                    F.ops[v - 1][2] = True
        E.ops.append([waits, fn, bool(signal), dma, tok])
        for r in reads:
            st = res.get(r)
            if st is None:
                st = [None, {}]
                res[r] = st
            if st[1].get(tok[0], 0) < tok[1]:
                st[1][tok[0]] = tok[1]
        for w in writes:
            res[w] = [tok, {}]
        return tok


class V:
    __slots__ = ("ap", "res")

    def __init__(self, ap, res):
        self.ap, self.res = ap, tuple(res)


class Prog:
    def __init__(self, ntok, depth, stage=9, dbg=()):
        self.stage, self.dbg = stage, set(dbg)
        self.ntok, self.depth = ntok, depth
        self.NT = ntok // TT
        self.NKT = ntok // 128
        self.NB = ntok // 512
        self.T = Tracker()
        self.nc = bass.Bass("TRN2", target_bir_lowering=False)
        self.semnames = []
        self.engs = {}
        for nm, pe in (("pe", True), ("act", False), ("dve", False), ("sp", False), ("pool", False)):
            self.engs[nm] = Eng(nm, self._newsem("e_" + nm), pe)
            self.T.engs_by_key[self.engs[nm].key] = self.engs[nm]
        for nm, n in (("sp", 28), ("pool", 28)):
            self.engs[nm].pool = [self._newsem(f"d_{nm}{i}") for i in range(n)]
        self.castsem = [self._newsem(f"cast{l}") for l in range(depth)]
        self.out_toks = []
        self.ps_i = 0
        self.declare()

    def _newsem(self, name):
        self.semnames.append(name)
        return len(self.semnames) - 1

    def declare(self):
        nc, NT, NKT, dp, ntok = self.nc, self.NT, self.NKT, self.depth, self.ntok
        di = lambda n, s, dt: nc.dram_tensor(n, s, dt, kind="ExternalInput").ap()
        ds = lambda n, s, dt: nc.dram_tensor(n, s, dt, kind=("ExternalOutput" if n in self.dbg else "Internal")).ap()
        self.xin = di("xin", [NT, 128, 16, 512], F32)
        self.yout = nc.dram_tensor("yout", [NT, 128, 16, 512], F32, kind="ExternalOutput").ap()
        self.wpack = di("wpack", [dp, 128, WTOT], F32)
        self.gains_d = di("gains", [128, dp * NG], F32)
        self.gmul_d = di("gmul", [128, dp * NG], F32)
        self.cos_d = di("cosT", [NT, 64, 512], F32)
        self.sin_d = di("sinT", [NT, 64, 512], F32)
        self.natab = di("natab", [dp, 8, 23, 64, 64], F32)
        self.namask = di("namask", [2, self.NB * 8 * 512], BF16)
        self.e2_d = di("e2", [2, 128], BF16)
        self.dist_d = di("dist", [128, N_DD, 512], BF16)
        self.mlab_d = di("mlab", [128, NT * NKT], F32)
        self.dilb_d = di("dilb", [128, NT * N_DD], F32)
        self.wb = [ds(f"wb{l}", [128, WTOT], BF16) for l in range(dp)]
        self.x1 = ds("x1", [NT, 128, 16, 512], F32)
        self.naq = ds("naq", [16, 128, ntok], BF16)
        self.nav = ds("nav", [8, NKT, 128, 128], BF16)
        self.dqk = ds("dqk", [24, 128, ntok], BF16)
        self.dvv = ds("dvv", [12, NKT, 128, 128], BF16)
        self.mq = ds("mq", [8, 192, ntok], BF16)
        self.mk = ds("mk", [8, 192, ntok], BF16)
        self.mv = ds("mv", [8, NKT, 128, 128], BF16)
        self.gat = ds("gat", [NT, 128, 48, 512], BF16)
        self.ya = ds("ya", [NT, 128, 20, 512], BF16)

    def ub(self, u0, n=1):
        return V(self.arena[:, u0 * 512:(u0 + n) * 512], [("u", u) for u in range(u0, u0 + n)])

    def uf(self, u0, n=2):
        return V(self.arena[:, u0 * 512:(u0 + n) * 512].bitcast(F32), [("u", u) for u in range(u0, u0 + n)])

    def ps(self):
        i = self.ps_i % 8
        self.ps_i += 1
        return V(self.psum[i][:, :], [("p", i)])

    def psb(self, i):
        return V(self.psum[i][:, :], [("p", i)])

    def op(self, en, fn, reads, writes, signal=False, extra=()):
        return self.T.op(self.engs[en], fn, reads, writes, signal=signal, extra=extra)

    def dma(self, en, out_ap, in_ap, reads, writes, extra=()):
        return self.T.op(self.engs[en], lambda e: e.dma_start(out=out_ap, in_=in_ap), reads, writes, dma=True, extra=extra)

    def mm(self, out, lhsT_ap, rhs_ap, reads, start, stop):
        o = out.ap
        self.op("pe", lambda e: e.matmul(o, lhsT_ap, rhs_ap, start=start, stop=stop), reads, out.res)

    def act(self, out, in_, func, bias=None, scale=None, extra_reads=()):
        o, i = out.ap, in_.ap
        kw = {}
        if bias is not None:
            kw["bias"] = bias
        if scale is not None:
            kw["scale"] = scale
        self.op("act", lambda e: e.activation(o, i, func, **kw), in_.res + tuple(extra_reads), out.res)

    def rsq(self, out, in_, addc):
        o, i = out.ap, in_.ap
        self.op("act", lambda e: e.activation(o, i, AF.Sqrt, bias=float(addc), scale=1.0), in_.res, out.res)
        self.op("dve", lambda e: e.reciprocal(o, o), out.res, out.res)

    def stt(self, out, in0, scalar, in1, op0, op1, en="dve"):
        o, a, b = out.ap, in0.ap, in1.ap
        self.op(en, lambda e: e.scalar_tensor_tensor(o, a, scalar, b, op0, op1), in0.res + in1.res, out.res)

    def tt(self, out, in0, in1, op, en="dve"):
        o, a, b = out.ap, in0.ap, in1.ap
        self.op(en, lambda e: e.tensor_tensor(o, a, b, op), in0.res + in1.res, out.res)

    def wstream_begin(self, l, names):
        self.ws_l = l
        self.ws_list = list(names)
        self.ws_loaded = 0
        self.ws_pos = 0

    def wtile(self, name, hold=0):
        assert self.ws_list[self.ws_pos] == name, (self.ws_list[self.ws_pos], name)
        PF = 2
        while self.ws_loaded < min(len(self.ws_list), self.ws_pos + PF + 1, self.ws_pos - hold + 4):
            nm = self.ws_list[self.ws_loaded]
            off, kc, ncol = _WOFF[nm]
            slot = self.ws_slot_i % 4
            self.ws_slot_i += 1
            dst = self.ub(self.U_WS + 16 * slot, 16)
            n = kc * ncol
            self.dma("sp", dst.ap[:, 0:n], self.wb[self.ws_l][:, off:off + n], [], dst.res,
                     extra=[(self.castsem[self.ws_l], self.cast_total[self.ws_l])])
            self.ws_slots[self.ws_loaded] = (slot, kc, ncol)
            self.ws_loaded += 1
        slot, kc, ncol = self.ws_slots[self.ws_pos]
        self.ws_pos += 1
        v = self.ub(self.U_WS + 16 * slot, 16)
        return V(v.ap[:, 0:kc * ncol].rearrange("p (k c) -> p k c", c=ncol), v.res)

    def rmsnorm_model(self, l, gcol0):
        ps = self.ps()
        for k in range(16):
            sq = self.ub(self.U_TMP + 2 + (k % 3))
            self.act(sq, self.X[k], AF.Square)
            self.mm(ps, self.ones[:, :], sq.ap, sq.res, k == 0, k == 15)
        r = self.uf(self.U_TMP + 0)
        self.rsq(r, ps, D * EPS)
        for k in range(16):
            g = self.gs[:, l * NG + gcol0 + k: l * NG + gcol0 + k + 1]
            self.stt(self.H[k], self.X[k], g, r, ALU.mult, ALU.mult)

    def ffn(self, l, f):
        gcol = G_F1 if f == "ffn1" else G_F2
        self.rmsnorm_model(l, gcol)
        hid = [self.ub(self.U_R + j) for j in range(44)]
        for t in range(11):
            wg = self.wtile(f"{f}g{t}")
            wu = self.wtile(f"{f}u{t}", hold=1)
            for oc in range(4):
                j = 4 * t + oc
                pg, pu = self.ps(), self.ps()
                for k in range(16):
                    self.mm(pg, wg.ap[:, k, oc * 128:(oc + 1) * 128], self.H[k].ap, wg.res + self.H[k].res, k == 0, k == 15)
                for k in range(16):
                    self.mm(pu, wu.ap[:, k, oc * 128:(oc + 1) * 128], self.H[k].ap, wu.res + self.H[k].res, k == 0, k == 15)
                sg = self.uf(self.U_TMP + 5 + 2 * (j % 2))
                self.act(sg, pg, AF.Silu)
                self.tt(hid[j], sg, pu, ALU.mult)
        for d in range(16):
            wd = self.wtile(f"{f}d{d}")
            pd = self.ps()
            for j in range(44):
                self.mm(pd, wd.ap[:, j, :], hid[j].ap, wd.res + hid[j].res, j == 0, j == 43)
            self.stt(self.X[d], pd, 0.5, self.X[d], ALU.mult, ALU.add)

    def headnorm_store(self, l, wname_fmt, ntiles, gq, gk, nq_heads, dst, tt):
        for t in range(ntiles):
            w = self.wtile(wname_fmt.format(t))
            stg = self.ub(self.U_TMP + 9 + 4 * (t % 2), 4)
            for oc in range(4):
                c = 4 * t + oc
                pz = self.ps()
                for k in range(16):
                    self.mm(pz, w.ap[:, k, oc * 128:(oc + 1) * 128], self.H[k].ap, w.res + self.H[k].res, k == 0, k == 15)
                sq = self.ub(self.U_TMP + 2 + (c % 3))
                self.act(sq, pz, AF.Square)
                pss = self.ps()
                self.mm(pss, self.ones[:, :], sq.ap, sq.res, True, True)
                r = self.uf(self.U_TMP + 17 + 2 * (c % 2))
                self.rsq(r, pss, 128 * EPS)
                gc = gq if c < nq_heads else gk
                g = self.gs[:, l * NG + gc: l * NG + gc + 1]
                so = V(stg.ap[:, oc * 512:(oc + 1) * 512], stg.res)
                self.stt(so, pz, g, r, ALU.mult, ALU.mult)
            src = stg.ap.rearrange("p (c t) -> p c t", t=512)
            dd = dst[4 * t:4 * t + 4, :, tt * 512:(tt + 1) * 512].rearrange("c p t -> p c t")
            self.dma("pool", dd, src, stg.res, [("d", dst.tensor.name, 4 * t + i, tt) for i in range(4)])

    def vtok_store(self, wname_fmt, ntiles, dst, tt):
        for t in range(ntiles):
            w = self.wtile(wname_fmt.format(t))
            for s in range(4):
                pv = self.ps()
                for k in range(16):
                    self.mm(pv, self.H[k].ap[:, s * 128:(s + 1) * 128], w.ap[:, k, :], w.res + self.H[k].res, k == 0, k == 15)
                stg = self.ub(self.U_TMP + 21 + (s % 2))
                self.act(stg, pv, AF.Copy)
                src = stg.ap.rearrange("p (h c) -> p h c", c=128)
                dd = dst[4 * t:4 * t + 4, 4 * tt + s, :, :].rearrange("h p c -> p h c")
                self.dma("pool", dd, src, stg.res, [("d", dst.tensor.name, 4 * t + i, 4 * tt + s) for i in range(4)])

    def mixer_in(self, l, tt):
        self.rmsnorm_model(l, G_MIX)
        R = self.U_R
        self.headnorm_store(l, "qkna{}", 4, G_NAQ, G_NAK, 8, self.naq, tt)
        self.vtok_store("vna{}", 2, self.nav, tt)
        cos = V(self.uf(self.U_TMP + 23).ap[0:64, :], self.uf(self.U_TMP + 23).res)
        sin = V(self.uf(self.U_TMP + 25).ap[0:64, :], self.uf(self.U_TMP + 25).res)
        self.dma("sp", cos.ap, self.cos_d[tt], [], cos.res)
        self.dma("sp", sin.ap, self.sin_d[tt], [], sin.res)
        w = self.wtile("cq")
        cqf = [self.uf(R + 2 * c) for c in range(4)]
        cqn = [self.ub(R + 8 + c) for c in range(4)]
        pss = self.ps()
        for c in range(4):
            pz = self.ps()
            for k in range(16):
                self.mm(pz, w.ap[:, k, c * 128:(c + 1) * 128], self.H[k].ap, w.res + self.H[k].res, k == 0, k == 15)
            self.act(cqf[c], pz, AF.Copy)
            sq = self.ub(self.U_TMP + 2 + (c % 3))
            self.act(sq, pz, AF.Square)
            self.mm(pss, self.ones[:, :], sq.ap, sq.res, c == 0, c == 3)
        r = self.uf(self.U_TMP + 17)
        self.rsq(r, pss, 512 * EPS)
        for c in range(4):
            g = self.gs[:, l * NG + G_CQ + c: l * NG + G_CQ + c + 1]
            self.stt(cqn[c], cqf[c], g, r, ALU.mult, ALU.mult)
        wn, wr, wt = self.wtile("uqn"), self.wtile("uqr", hold=1), self.wtile("uqt", hold=2)
        for h in range(8):
            self.mla_head(l, tt, h, cqn, 4, wn, wr, wt, None, cos, sin, G_MQ, self.mq)
        w = self.wtile("ckv")
        ckf = [self.uf(R + 2 * c) for c in range(2)]
        ckn = [self.ub(R + 12 + c) for c in range(2)]
        pss = self.ps()
        for c in range(2):
            pz = self.ps()
            for k in range(16):
                self.mm(pz, w.ap[:, k, c * 128:(c + 1) * 128], self.H[k].ap, w.res + self.H[k].res, k == 0, k == 15)
            self.act(ckf[c], pz, AF.Copy)
            sq = self.ub(self.U_TMP + 2 + (c % 3))
            self.act(sq, pz, AF.Square)
            self.mm(pss, self.ones[:, :], sq.ap, sq.res, c == 0, c == 1)
        r = self.uf(self.U_TMP + 17)
        self.rsq(r, pss, 256 * EPS)
        for c in range(2):
            g = self.gs[:, l * NG + G_CKV + c: l * NG + G_CKV + c + 1]
            self.stt(ckn[c], ckf[c], g, r, ALU.mult, ALU.mult)
        w = self.wtile("krr")
        pk, pt = self.ps(), self.ps()
        pk64 = V(pk.ap[0:64, :], pk.res)
        pt64 = V(pt.ap[0:64, :], pt.res)
        for k in range(16):
            self.mm(pk64, w.ap[:, k, 0:64], self.H[k].ap, w.res + self.H[k].res, k == 0, k == 15)
        for k in range(16):
            self.mm(pt64, w.ap[:, k, 64:128], self.H[k].ap, w.res + self.H[k].res, k == 0, k == 15)
        krf = self._rope(pk64, pt64, cos, sin, R + 14)
        sqkr = V(self.ub(R + 20).ap[0:64, :], self.ub(R + 20).res)
        self.act(sqkr, krf, AF.Square)
        wk, wv = self.wtile("ukvk"), self.wtile("ukvv", hold=1)
        for h in range(8):
            self.mla_head(l, tt, h, ckn, 2, wk, None, None, (krf, sqkr), cos, sin, G_MK, self.mk)
        for s in range(4):
            for cg in range(2):
                pv = self.ps()
                for c in range(2):
                    self.mm(pv, ckn[c].ap[:, s * 128:(s + 1) * 128], wv.ap[:, c, cg * 512:(cg + 1) * 512],
                            wv.res + ckn[c].res, c == 0, c == 1)
                stg = self.ub(self.U_TMP + 21 + ((2 * s + cg) % 2))
                self.act(stg, pv, AF.Copy)
                src = stg.ap.rearrange("p (h c) -> p h c", c=128)
                dd = self.mv[4 * cg:4 * cg + 4, 4 * tt + s, :, :].rearrange("h p c -> p h c")
                self.dma("pool", dd, src, stg.res, [("d", "mv", 4 * cg + i, 4 * tt + s) for i in range(4)])
        self.headnorm_store(l, "qkdil{}", 6, G_DQ, G_DK, 12, self.dqk, tt)
        self.vtok_store("vdil{}", 3, self.dvv, tt)
        for t in range(12):
            w = self.wtile(f"gate{t}")
            stg = self.ub(self.U_TMP + 9 + 4 * (t % 2), 4)
            for oc in range(4):
                pz = self.ps()
                for k in range(16):
                    self.mm(pz, w.ap[:, k, oc * 128:(oc + 1) * 128], self.H[k].ap, w.res + self.H[k].res, k == 0, k == 15)
                self.act(V(stg.ap[:, oc * 512:(oc + 1) * 512], stg.res), pz, AF.Sigmoid)
            self.dma("pool", self.gat[tt, :, 4 * t:4 * t + 4, :], stg.ap.rearrange("p (c t) -> p c t", t=512),
                     stg.res, [("d", "gat", tt, t)])
        xall = V(self.arena[:, self.U_X * 512:(self.U_X + 32) * 512].bitcast(F32).rearrange("p (k t) -> p k t", t=512),
                 [("u", u) for u in range(self.U_X, self.U_X + 32)])
        self.dma("pool", self.x1[tt], xall.ap, xall.res, [("d", "x1", tt)])

    def _rope(self, p_raw, p_rot, cos, sin, u0):
        t1 = V(self.uf(u0).ap[0:64, :], self.uf(u0).res)
        t2 = V(self.uf(u0 + 2).ap[0:64, :], self.uf(u0 + 2).res)
        o = V(self.uf(u0 + 4).ap[0:64, :], self.uf(u0 + 4).res)
        self.tt(t1, p_raw, cos, ALU.mult)
        self.tt(t2, p_rot, sin, ALU.mult)
        self.tt(o, t1, t2, ALU.add)
        return o

    def mla_head(self, l, tt, h, xin, nkc, wn, wr, wt, kshared, cos, sin, gcol, dst):
        R = self.U_R
        pn = self.ps()
        for c in range(nkc):
            self.mm(pn, wn.ap[:, c, h * 128:(h + 1) * 128], xin[c].ap, wn.res + xin[c].res, c == 0, c == nkc - 1)
        sqn = self.ub(self.U_TMP + 2 + (h % 3))
        self.act(sqn, pn, AF.Square)
        if kshared is None:
            pr, pt = self.ps(), self.ps()
            pr64, pt64 = V(pr.ap[0:64, :], pr.res), V(pt.ap[0:64, :], pt.res)
            for c in range(nkc):
                self.mm(pr64, wr.ap[:, c, h * 64:(h + 1) * 64], xin[c].ap, wr.res + xin[c].res, c == 0, c == nkc - 1)
            for c in range(nkc):
                self.mm(pt64, wt.ap[:, c, h * 64:(h + 1) * 64], xin[c].ap, wt.res + xin[c].res, c == 0, c == nkc - 1)
            rf = self._rope(pr64, pt64, cos, sin, R + 22 + 6 * (h % 2))
            sqr = V(self.ub(R + 34 + (h % 2)).ap[0:64, :], self.ub(R + 34 + (h % 2)).res)
            self.act(sqr, rf, AF.Square)
        else:
            rf, sqr = kshared
        pss = self.ps()
        self.mm(pss, self.ones[:, :], sqn.ap, sqn.res, True, False)
        self.mm(pss, self.ones[0:64, :], sqr.ap, sqr.res, False, True)
        r = self.uf(R + 36 + 2 * (h % 2))
        self.rsq(r, pss, 192 * EPS)
        g1 = self.gs[:, l * NG + gcol: l * NG + gcol + 1]
        g2 = self.gs[0:64, l * NG + gcol + 1: l * NG + gcol + 2]
        s1 = self.ub(R + 40 + (h % 2))
        s2 = V(self.ub(R + 42 + (h % 2)).ap[0:64, :], self.ub(R + 42 + (h % 2)).res)
        self.stt(s1, pn, g1, r, ALU.mult, ALU.mult)
        self.stt(s2, rf, g2, V(r.ap[0:64, :], r.res), ALU.mult, ALU.mult)
        nm = dst.tensor.name
        self.dma("pool", dst[h, 0:128, tt * 512:(tt + 1) * 512], s1.ap, s1.res, [("d", nm, h, tt, 0)])
        self.dma("pool", dst[h, 128:192, tt * 512:(tt + 1) * 512], s2.ap, s2.res, [("d", nm, h, tt, 1)])

    def mixer_out(self, l, tt):
        R = self.U_R
        xall = V(self.arena[:, self.U_X * 512:(self.U_X + 32) * 512].bitcast(F32).rearrange("p (k t) -> p k t", t=512),
                 [("u", u) for u in range(self.U_X, self.U_X + 32)])
        self.dma("sp", xall.ap, self.x1[tt], [("d", "x1", tt)], xall.res)
        ya = self.ub(R, 20)
        yar = [("d", "ya", tt, c) for c in range(20)]
        self.dma("sp", ya.ap.rearrange("p (c t) -> p c t", t=512), self.ya[tt], yar, ya.res)
        m = [self.ub(R + 20 + d) for d in range(16)]
        kcs = ((0, 8), (8, 8), (16, 4))
        for t in range(8):
            w = self.wtile(f"mo{t}")
            for oc in range(2):
                d = 2 * t + oc
                gt = self.ub(R + 36 + 3 * (d % 2), 3)
                ga = gt.ap.rearrange("p (c t) -> p c t", t=512)
                for b in range(3):
                    self.dma("sp", ga[:, b, :], self.gat[tt, :, 16 * b + d, :], [("d", "gat", tt, (16 * b + d) // 4)], gt.res)
                pss = []
                for b, (k0, nk) in enumerate(kcs):
                    p = self.ps()
                    for k in range(nk):
                        self.mm(p, w.ap[:, k0 + k, oc * 128:(oc + 1) * 128], ya.ap[:, (k0 + k) * 512:(k0 + k + 1) * 512],
                                w.res + ya.res, k == 0, k == nk - 1)
                    pss.append(p)
                t1 = self.uf(self.U_TMP + 5)
                t2 = self.uf(self.U_TMP + 7)
                self.tt(t1, pss[0], V(ga[:, 0, :], gt.res), ALU.mult)
                self.tt(t2, pss[1], V(ga[:, 1, :], gt.res), ALU.mult)
                self.tt(t1, t1, t2, ALU.add)
                self.tt(t2, pss[2], V(ga[:, 2, :], gt.res), ALU.mult)
                self.tt(m[d], t1, t2, ALU.add)
        for t in range(4):
            w = self.wtile(f"wo{t}")
            for oc in range(4):
                d = 4 * t + oc
                p = self.ps()
                for k in range(16):
                    self.mm(p, w.ap[:, k, oc * 128:(oc + 1) * 128], m[k].ap, w.res + m[k].res, k == 0, k == 15)
                self.tt(self.X[d], self.X[d], p, ALU.add)

    def attn_core(self, O, Dn, v_ap, v_res, pb, first, last):
        self.mm(O, v_ap, pb.ap, v_res + pb.res, first, last)
        self.mm(Dn, self.ones[:, :], pb.ap, pb.res, first, last)

    def attn_finish(self, O, Dn, qt, chunk, ui):
        rinv = self.uf(self.A_MISC + 0)
        o, i = rinv.ap, Dn.ap
        self.op("dve", lambda e: e.reciprocal(o, i), Dn.res, rinv.res)
        ys = self.ub(self.A_MISC + 2 + (ui % 2))
        self.tt(ys, O, rinv, ALU.mult)
        self.dma("pool", self.ya[qt, :, chunk, :], ys.ap, ys.res, [("d", "ya", qt, chunk)])

    def load_kv(self, u0, ksrc_ap, kd, vsrc, vd, parts=128):
        k = self.ub(u0, 16)
        self.dma("sp", k.ap[0:parts, 0:self.ntok], ksrc_ap, kd, k.res)
        v = self.ub(u0 + 16, 16)
        va = v.ap[:, 0:self.NKT * 128].rearrange("p (t c) -> p t c", c=128)
        self.dma("sp", va, vsrc.rearrange("t p c -> p t c"), vd, v.res)
        return k, V(va, v.res)

    def attn_na(self, l):
        NB, NKT, ntok = self.NB, self.NKT, self.ntok
        A = 0
        self.A_MISC = 130
        e2 = self.e2s
        for h in range(8):
            hb = (h % 2) * 32
            kd = [("d", "naq", 8 + h, t) for t in range(self.NT)]
            vd = [("d", "nav", h, t) for t in range(NKT)]
            k, v = self.load_kv(A + hb, self.naq[8 + h], kd, self.nav[h], vd)
            C = self.uf(64, 32)
            Ca = C.ap.rearrange("p (c t) -> p c t", t=512)
            for c in range(8):
                for i in range(2):
                    e0 = 15 - 2 * c - i
                    src = self.natab[l, h, e0:e0 + 8, :, :].rearrange("e k q -> k e q")
                    dst = Ca[i * 64:(i + 1) * 64, c, :].rearrange("p (e q) -> p e q", q=64)
                    self.dma("sp", dst, src, [], C.res)
            for b in range(NB):
                q = self.ub(96 + (b % 2))
                self.dma("sp", q.ap, self.naq[h, :, b * 512:(b + 1) * 512], [("d", "naq", h, b)], q.res)
                mr = self.ub(98 + 8 * (b % 2), 8)
                self.dma("sp", mr.ap[0:2, :], self.namask[:, b * 4096:(b + 1) * 4096], [], mr.res)
                O, Dn = self.psb(2 * (b % 2)), self.psb(2 * (b % 2) + 1)
                cs = [c for c in range(8) if 0 <= 4 * b - 2 + c < NKT]
                for ci, c in enumerate(cs):
                    kt = 4 * b - 2 + c
                    S = self.psb(4 + ci % 4)
                    self.mm(S, k.ap[:, kt * 128:(kt + 1) * 128], q.ap, k.res + q.res, True, False)
                    self.mm(S, e2[0:2, :], mr.ap[0:2, c * 512:(c + 1) * 512], mr.res, False, True)
                    sc = self.uf(114 + 2 * (ci % 3))
                    self.tt(sc, S, V(Ca[:, c, :], C.res), ALU.add)
                    pb = self.ub(120 + (ci % 3))
                    self.act(pb, sc, AF.Exp)
                    self.attn_core(O, Dn, v.ap[:, kt, :], v.res, pb, ci == 0, ci == len(cs) - 1)
                self.attn_finish(O, Dn, b, h, b)

    def attn_mla(self, l):
        NT, NKT = self.NT, self.NKT
        self.A_MISC = 130
        bc = self.uf(140, 8)
        self.dma("sp", bc.ap[:, 0:NT * NKT], self.mlab_d, [], bc.res)
        for h in range(8):
            hb = (h % 2) * 48
            kd0 = [("d", "mk", h, t, 0) for t in range(NT)]
            kd1 = [("d", "mk", h, t, 1) for t in range(NT)]
            vd = [("d", "mv", h, t) for t in range(NKT)]
            kA, v = self.load_kv(hb, self.mk[h, 0:128, :], kd0, self.mv[h], vd)
            kB = self.ub(hb + 32, 16)
            self.dma("sp", kB.ap[0:64, 0:self.ntok], self.mk[h, 128:192, :], kd1, kB.res)
            for qt in range(NT):
                qA = self.ub(96 + 2 * (qt % 2))
                qB = self.ub(97 + 2 * (qt % 2))
                self.dma("sp", qA.ap, self.mq[h, 0:128, qt * 512:(qt + 1) * 512], [("d", "mq", h, qt, 0)], qA.res)
                self.dma("sp", qB.ap[0:64, :], self.mq[h, 128:192, qt * 512:(qt + 1) * 512], [("d", "mq", h, qt, 1)], qB.res)
                O, Dn = self.psb(2 * (qt % 2)), self.psb(2 * (qt % 2) + 1)
                for kt in range(NKT):
                    S = self.psb(4 + kt % 4)
                    self.mm(S, kA.ap[:, kt * 128:(kt + 1) * 128], qA.ap, kA.res + qA.res, True, False)
                    self.mm(S, kB.ap[0:64, kt * 128:(kt + 1) * 128], qB.ap[0:64, :], kB.res + qB.res, False, True)
                    pb = self.ub(120 + (kt % 4))
                    col = qt * NKT + kt
                    self.act(pb, S, AF.Exp, bias=bc.ap[:, col:col + 1], extra_reads=bc.res)
                    self.attn_core(O, Dn, v.ap[:, kt, :], v.res, pb, kt == 0, kt == NKT - 1)
                self.attn_finish(O, Dn, qt, 8 + h, qt)

    def attn_dil(self, l):
        NT, NKT = self.NT, self.NKT
        self.A_MISC = 150
        dist = self.ub(96, N_DD)
        da = dist.ap.rearrange("p (c t) -> p c t", t=512)
        self.dma("sp", da, self.dist_d, [], dist.res)
        bc = self.uf(140, 4)
        self.dma("sp", bc.ap[:, 0:NT * N_DD], self.dilb_d, [], bc.res)
        for j in range(4):
            ks, vs = [], []
            for g in range(3):
                hd = 4 * g + j
                kd = [("d", "dqk", 12 + hd, t) for t in range(NT)]
                vd = [("d", "dvv", hd, t) for t in range(NKT)]
                k, v = self.load_kv(32 * g, self.dqk[12 + hd], kd, self.dvv[hd], vd)
                ks.append(k)
                vs.append(v)
            for qt in range(NT):
                q = self.ub(156 + 3 * (qt % 2), 3)
                for g in range(3):
                    self.dma("sp", q.ap[:, g * 512:(g + 1) * 512], self.dqk[4 * g + j, :, qt * 512:(qt + 1) * 512],
                             [("d", "dqk", 4 * g + j, qt)], q.res)
                O, Dn = self.psb(2 * (qt % 2)), self.psb(2 * (qt % 2) + 1)
                pairs = []
                idx = 0
                for g in range(3):
                    for dl in DIL_DELTAS[g]:
                        kt = 4 * qt + dl
                        if 0 <= kt < NKT:
                            pairs.append((g, idx, kt))
                        idx += 1
                for pi, (g, idx, kt) in enumerate(pairs):
                    S = self.psb(4 + pi % 4)
                    self.mm(S, ks[g].ap[:, kt * 128:(kt + 1) * 128], q.ap[:, g * 512:(g + 1) * 512], ks[g].res + q.res, True, True)
                    sc = self.uf(162 + 2 * (pi % 3))
                    self.stt(sc, V(da[:, idx, :], dist.res), -SLOPES[4 * g + j], S, ALU.mult, ALU.add)
                    pb = self.ub(168 + (pi % 3))
                    col = qt * N_DD + idx
                    self.act(pb, sc, AF.Exp, bias=bc.ap[:, col:col + 1], extra_reads=bc.res)
                    self.attn_core(O, Dn, vs[g].ap[:, kt, :], vs[g].res, pb, pi == 0, pi == len(pairs) - 1)
                self.attn_finish(O, Dn, qt, 16 + j, qt)

    def record(self):
        nc, dp, NT = self.nc, self.depth, self.NT
        self.U_X, self.U_H, self.U_R, self.U_WS, self.U_TMP = 0, 32, 48, 92, 156
        self.X = [self.uf(self.U_X + 2 * k) for k in range(16)]
        self.H = [self.ub(self.U_H + k) for k in range(16)]
        self.ws_slot_i = 0
        self.ws_slots = {}
        CH = 16384
        self.cast_total = [16 * len(range(0, WTOT, CH)) for _ in range(dp)]
        pool = self.engs["pool"]

        def cast_layer(l):
            for c0 in range(0, WTOT, CH):
                c1 = min(WTOT, c0 + CH)
                o_, i_ = self.wb[l][:, c0:c1], self.wpack[l, :, c0:c1]
                sk = self.castsem[l]
                pool.ops.append([[], (lambda e, o_=o_, i_=i_: e.dma_start(out=o_, in_=i_)), False, True, (sk, 0)])
        self.cast_layer = cast_layer
        cast_layer(0)
        onesv = V(self.ones[:, :], [("c", "ones")])
        o = self.ones[:, :]
        self.op("dve", lambda e: e.memset(o, 1.0), [], onesv.res)
        gsv = V(self.gs[:, :], [("c", "gs")])
        gmv = V(self.gm[:, :], [("c", "gm")])
        self.dma("sp", self.gs[:, :], self.gains_d, [], gsv.res)
        self.dma("sp", self.gm[:, :], self.gmul_d, [], gmv.res)
        self.dma("sp", self.e2s[:, :], self.e2_d, [], [("c", "e2")])
        self.tt(gsv, gsv, gmv, ALU.mult)
        ctoks = [self.T.res[("c", n)][0] for n in ("ones", "gs", "e2")]
        for en in ("pe", "act", "dve"):
            self.op(en, None, [], [], signal=False, extra=ctoks)
        for l in range(dp):
            names = []
            names += [f"ffn1{x}{t}" for t in range(11) for x in "gu"] + [f"ffn1d{d}" for d in range(16)]
            mixin = [f"qkna{t}" for t in range(4)] + ["vna0", "vna1", "cq", "uqn", "uqr", "uqt", "ckv", "krr", "ukvk", "ukvv"] + \
                    [f"qkdil{t}" for t in range(6)] + [f"vdil{t}" for t in range(3)] + [f"gate{t}" for t in range(12)]
            if self.stage > 1:
                names += mixin
            if self.stage < 1:
                names = []
            self.wstream_begin(l, names * NT)
            for tt in range(NT):
                xall = V(self.arena[:, self.U_X * 512:(self.U_X + 32) * 512].bitcast(F32).rearrange("p (k t) -> p k t", t=512),
                         [("u", u) for u in range(self.U_X, self.U_X + 32)])
                if l == 0:
                    self.dma("sp", xall.ap, self.xin[tt], [], xall.res)
                else:
                    self.dma("sp", xall.ap, self.x1[tt], [("d", "x1", tt)], xall.res)
                if self.stage >= 1:
                    self.ffn(l, "ffn1")
                elif self.stage == -1:
                    self.rmsnorm_model(l, G_F1)
                if self.stage <= 1:
                    tok = self.dma("pool", self.yout[tt], xall.ap, xall.res, [("d", "yout", tt)])
                    self.out_toks.append(tok)
                    continue
                self.mixer_in(l, tt)
                if self.stage < 6:
                    tok = self.dma("pool", self.yout[tt], xall.ap, xall.res, [("d", "yout", tt)])
                    self.out_toks.append(tok)
            if self.stage <= 1:
                break
            if l + 1 < dp:
                self.cast_layer(l + 1)
            if self.stage >= 3:
                self.attn_na(l)
            if self.stage >= 4:
                self.attn_mla(l)
            if self.stage >= 5:
                self.attn_dil(l)
            if self.stage < 6:
                self.op("pool", None, [], [], signal=False,
                        extra=[(sk, 16 * ((self.engs["pool"].dma_i - 1 - i) // len(self.engs["pool"].pool) + 1))
                               for i, sk in enumerate(self.engs["pool"].pool) if self.engs["pool"].dma_i > i])
                break
            names = [f"mo{t}" for t in range(8)] + [f"wo{t}" for t in range(4)] + \
                    [f"ffn2{x}{t}" for t in range(11) for x in "gu"] + [f"ffn2d{d}" for d in range(16)]
            self.wstream_begin(l, names * NT)
            for tt in range(NT):
                self.mixer_out(l, tt)
                self.ffn(l, "ffn2")
                xall = V(self.arena[:, self.U_X * 512:(self.U_X + 32) * 512].bitcast(F32).rearrange("p (k t) -> p k t", t=512),
                         [("u", u) for u in range(self.U_X, self.U_X + 32)])
                if l == dp - 1:
                    tok = self.dma("pool", self.yout[tt], xall.ap, xall.res, [("d", "yout", tt)])
                    self.out_toks.append(tok)
                else:
                    self.dma("pool", self.x1[tt], xall.ap, xall.res, [("d", "x1", tt)])
        self.op("pool", None, [], [], signal=False, extra=self.out_toks)

    def build(self):
        nc = self.nc
        from contextlib import ExitStack
        with ExitStack() as es:
            self.arena = es.enter_context(nc.sbuf_tensor("arena", [128, NU * 512], BF16))
            self.ones = es.enter_context(nc.sbuf_tensor("ones", [128, 128], BF16))
            self.gs = es.enter_context(nc.sbuf_tensor("gs", [128, self.depth * NG], F32))
            self.gm = es.enter_context(nc.sbuf_tensor("gm", [128, self.depth * NG], F32))
            self.e2s = es.enter_context(nc.sbuf_tensor("e2s", [2, 128], BF16))
            self.psum = [es.enter_context(nc.psum_tensor(f"ps{i}", [128, 512], F32)) for i in range(8)]
            sems = [es.enter_context(nc.semaphore(n)) for n in self.semnames]
            self.record()
            block = es.enter_context(nc.Block())
            engs = self.engs

            ebk = self.T.engs_by_key
            cnts = {}
            for E in engs.values():
                c, arr = 0, []
                for o in E.ops:
                    if o[2] and not o[3]:
                        c += 1
                    arr.append(c)
                cnts[E.key] = arr

            def emit(E, e):
                for (waits, fn, sig, is_dma, tok) in E.ops:
                    for (s_, v) in waits:
                        if s_ in ebk:
                            e.wait_ge(sems[s_], cnts[s_][v - 1])
                        else:
                            e.wait_ge(sems[s_], v)
                    if fn is None:
                        continue
                    ins = fn(e)
                    if is_dma:
                        ins.then_inc(sems[tok[0]], 16)
                    elif sig:
                        ins.then_inc(sems[E.key], 1)

            @block.tensor
            def _(e):
                emit(engs["pe"], e)

            @block.scalar
            def _(e):
                emit(engs["act"], e)

            @block.vector
            def _(e):
                emit(engs["dve"], e)

            @block.sync
            def _(e):
                emit(engs["sp"], e)

            @block.gpsimd
            def _(e):
                emit(engs["pool"], e)
        return nc


def core_tables(seqs, ntok):
    NT, NKT, NB = ntok // 512, ntok // 128, ntok // 512
    pos = np.concatenate([np.arange(s) for s in seqs])
    sid = np.concatenate([np.full(s, i) for i, s in enumerate(seqs)])
    sstart = np.concatenate([np.full(s, st) for s, st in zip(seqs, np.cumsum([0] + list(seqs[:-1])))])
    half = 32
    inv = (10000.0 ** (-np.arange(half, dtype=np.float32) / half)).astype(np.float32)
    ang = pos.astype(np.float32)[:, None] * inv[None, :]
    cos, sin = np.cos(ang).astype(np.float32), np.sin(ang).astype(np.float32)
    cosT = np.concatenate([cos, cos], axis=1).T
    sinT = np.concatenate([-sin, sin], axis=1).T
    cosT = np.ascontiguousarray(cosT.reshape(64, NT, 512).transpose(1, 0, 2))
    sinT = np.ascontiguousarray(sinT.reshape(64, NT, 512).transpose(1, 0, 2))
    nrows = ntok // 64
    row_sid = sid[::64]
    row_start = sstart[::64] // 64
    row_len = np.array([seqs[i] // 64 for i in row_sid])
    nam = np.full((2, NB, 8, 8, 64), NEG, np.float32)
    for b in range(NB):
        for c in range(8):
            for i in range(2):
                kr = 8 * b - 4 + 2 * c + i
                if kr < 0 or kr >= nrows:
                    continue
                for j in range(8):
                    qr = 8 * b + j
                    if row_sid[kr] != row_sid[qr]:
                        continue
                    qrel = qr - row_start[qr]
                    r0 = min(max(qrel - 4, 0), row_len[qr] - 8)
                    krel = kr - row_start[qr]
                    if r0 <= krel < r0 + 8:
                        nam[i, b, c, j, :] = 0.0
    namask = nam.reshape(2, NB * 8 * 512).astype(NPBF)
    mlab = np.full((NT, NKT), NEG, np.float32)
    for qt in range(NT):
        for kt in range(NKT):
            if sid[qt * 512] == sid[kt * 128]:
                mlab[qt, kt] = 0.0
    mlab = np.ascontiguousarray(np.broadcast_to(mlab.reshape(1, -1), (128, NT * NKT)))
    dilb = np.full((NT, N_DD), NEG, np.float32)
    for qt in range(NT):
        idx = 0
        for g in range(3):
            for dl in DIL_DELTAS[g]:
                kt = 4 * qt + dl
                if 0 <= kt < NKT and sid[qt * 512] == sid[kt * 128]:
                    dilb[qt, idx] = 0.0
                idx += 1
    dilb = np.ascontiguousarray(np.broadcast_to(dilb.reshape(1, -1), (128, NT * N_DD)))
    return dict(cosT=cosT, sinT=sinT, namask=namask, mlab=mlab, dilb=dilb)


def const_tables():
    p = np.arange(128)[:, None]
    f = np.arange(512)[None, :]
    dist = np.empty((128, N_DD, 512), np.float32)
    idx = 0
    for g in range(3):
        d = DIL_D[g]
        for dl in DIL_DELTAS[g]:
            df = f - 128 * dl - p
            ok = (np.abs(df) <= 64 * d) & (df % d == 0)
            dist[:, idx, :] = np.where(ok, np.abs(df), BIGD)
            idx += 1
    e2 = np.zeros((2, 128), np.float32)
    e2[0, :64] = 1.0
    e2[1, 64:] = 1.0
    return dict(dist=dist.astype(NPBF), e2=e2.astype(NPBF))


def gains_tables(inp, depth):
    g = np.zeros((128, depth * NG), np.float32)
    m = np.ones((128, depth * NG), np.float32)
    for l in range(depth):
        b = l * NG
        for nm, c0 in (("ffn1_norm", G_F1), ("mix_norm", G_MIX), ("ffn2_norm", G_F2)):
            g[:, b + c0:b + c0 + 16] = inp[nm][l].reshape(16, 128).T
            m[:, b + c0:b + c0 + 16] = math.sqrt(2048.0)
        g[:, b + G_NAQ] = inp["na_q_norm"][l]
        g[:, b + G_NAK] = inp["na_k_norm"][l]
        m[:, b + G_NAK] = math.sqrt(128.0)
        g[:, b + G_CQ:b + G_CQ + 4] = inp["mla_cq_norm"][l].reshape(4, 128).T
        m[:, b + G_CQ:b + G_CQ + 4] = math.sqrt(512.0)
        g[:, b + G_CKV:b + G_CKV + 2] = inp["mla_ckv_norm"][l].reshape(2, 128).T
        m[:, b + G_CKV:b + G_CKV + 2] = 16.0
        g[:, b + G_MQ] = inp["mla_q_norm"][l][:128]
        g[:64, b + G_MQ + 1] = inp["mla_q_norm"][l][128:]
        g[:, b + G_MK] = inp["mla_k_norm"][l][:128]
        g[:64, b + G_MK + 1] = inp["mla_k_norm"][l][128:]
        m[:, b + G_MK:b + G_MK + 2] = math.sqrt(192.0)
        g[:, b + G_DQ] = inp["dil_q_norm"][l]
        g[:, b + G_DK] = inp["dil_k_norm"][l]
        m[:, b + G_DK] = math.sqrt(128.0)
    return g, m


def na_table(rpb, depth):
    kcol = np.arange(64)[:, None]
    qcol = np.arange(64)[None, :]
    cstart = np.clip(qcol - 8, 0, 48)
    ok = (kcol >= cstart) & (kcol < cstart + 16)
    dci = np.clip(kcol - qcol, -15, 15) + 15
    out = np.full((depth, 8, 23, 64, 64), NEG, np.float32)
    for e in range(15):
        dr = 7 - e
        vals = rpb[:depth, :, dr + 7, :][:, :, dci]
        out[:, :, e + 4] = np.where(ok[None, None], vals, np.float32(NEG))
    return out


def to_fm(x, ntok):
    NT = ntok // 512
    return np.ascontiguousarray(x.reshape(NT, 512, 16, 128).transpose(0, 3, 2, 1))


def from_fm(y, ntok):
    NT = ntok // 512
    return np.ascontiguousarray(y.transpose(0, 3, 2, 1).reshape(ntok, 2048))


def run_cores(inp, core_x, core_seqs, ntok, depth, stage=9, dbg=(), raw=None):
    prog = Prog(ntok, depth, stage, dbg)
    nc = prog.build()
    wpack = np.stack([pack_weights_layer(inp, l) for l in range(depth)])
    g, m = gains_tables(inp, depth)
    ct = const_tables()
    natab = na_table(np.asarray(inp["na_rpb"]), depth)
    in_maps = []
    for x, seqs in zip(core_x, core_seqs):
        t = core_tables(seqs, ntok)
        in_maps.append(dict(xin=to_fm(x, ntok), wpack=wpack, gains=g, gmul=m, cosT=t["cosT"], sinT=t["sinT"],
                            natab=natab, namask=t["namask"], e2=ct["e2"], dist=ct["dist"],
                            mlab=t["mlab"], dilb=t["dilb"]))
    res = run_bass_kernel_spmd(nc, in_maps, core_ids=list(range(len(in_maps))))
    if raw is not None:
        raw.extend(res.results)
    return [from_fm(r["yout"], ntok) for r in res.results]


def _core_split(inp):
    xp, xs = inp["x_prompt"], inp["x_sample"]
    core_x, core_seqs = [], []
    for c in range(4):
        core_x.append(xp[4 * c:4 * c + 4].reshape(NTOK_FULL, D))
        core_seqs.append([2048] * 4)
    for c in range(4):
        core_x.append(xs[c].reshape(NTOK_FULL, D))
        core_seqs.append([8192])
    return core_x, core_seqs


def kernel_fused(**inputs):
    inp = {k: np.asarray(v) for k, v in inputs.items()}
    core_x, core_seqs = _core_split(inp)
    outs = run_cores(inp, core_x, core_seqs, NTOK_FULL, DEPTH_FULL)
    yp = np.stack(outs[:4]).reshape(16, 2048, D).astype(np.float32)
    ys = np.stack(outs[4:]).reshape(4, 8192, D).astype(np.float32)
    return (yp, ys)


def kernel(**inputs):
    inp = {k: np.asarray(v) for k, v in inputs.items()}
    core_x, core_seqs = _core_split(inp)
    prog = Prog(NTOK_FULL, 1)
    nc = prog.build()
    ct = const_tables()
    tabs = [core_tables(seqs, NTOK_FULL) for seqs in core_seqs]
    xcur = [to_fm(x, NTOK_FULL) for x in core_x]
    wnames = [k for k in inp if k not in ("x_prompt", "x_sample")]
    for l in range(DEPTH_FULL):
        inp_l = {k: inp[k][l:l + 1] for k in wnames}
        wpack = pack_weights_layer(inp_l, 0)[None]
        g, m = gains_tables(inp_l, 1)
        natab = na_table(inp_l["na_rpb"], 1)
        in_maps = []
        for c in range(N_CORES):
            t = tabs[c]
            in_maps.append(dict(xin=xcur[c], wpack=wpack, gains=g, gmul=m, cosT=t["cosT"], sinT=t["sinT"],
                                natab=natab, namask=t["namask"], e2=ct["e2"], dist=ct["dist"],
                                mlab=t["mlab"], dilb=t["dilb"]))
        res = run_bass_kernel_spmd(nc, in_maps, core_ids=list(range(N_CORES)))
        xcur = [np.asarray(r["yout"]) for r in res.results]
        del in_maps, wpack
    outs = [from_fm(x, NTOK_FULL) for x in xcur]
    yp = np.stack(outs[:4]).reshape(16, 2048, D).astype(np.float32)
    ys = np.stack(outs[4:]).reshape(4, 8192, D).astype(np.float32)
    return (yp, ys)
```
